# Optimizing a Trainium2 kernel written in Bass

```python
import jax, jax.numpy as jnp
from jax import lax
import numpy as np

D_MODEL = 1024
BATCH = 4
SEQ = 4096
DEPTH = 4

D_MIX = D_MODEL
HG_HEADS = 4
HG_DK = 128
HG_DV = 128
HG_W = HG_HEADS * HG_DV
HG_CHUNK = 64
NSA_HEADS = 8
NSA_KV_HEADS = 2
NSA_DH = 64
NSA_G = NSA_HEADS // NSA_KV_HEADS
NSA_W = NSA_HEADS * NSA_DH
KV_W = NSA_KV_HEADS * NSA_DH
CMP_BLOCK = 32
CMP_STRIDE = 16
CMP_HIDDEN = 128
SLC_BLOCK = 64
SLC_TOPK = 16
SLC_Q_BLOCK = 64
WINDOW = 512
Q_BLOCK = 128
FORCE_SCORE = 1e4
ROPE_THETA = 500000.0
ROPE_DIM = NSA_DH // 4
PLE_DIM = 256
RMS_EPS = 1e-6
IN_SPLITS = (HG_W,) * 5 + (NSA_W,) + (KV_W,) * 6 + (3 * NSA_HEADS, NSA_W)
IN_COLS = 5 * HG_W + NSA_W + 6 * KV_W + 3 * NSA_HEADS + NSA_W

kernel_name = "hymba_hgrn2_nsa_trunk"


def rmsnorm(x, g):
    xf = x.astype(jnp.float32)
    var = jnp.mean(xf * xf, axis=-1, keepdims=True)
    return (xf * lax.rsqrt(var + RMS_EPS)).astype(x.dtype) * g


def masked_softmax(s, mask):
    s = jnp.where(mask, s.astype(jnp.float32), -jnp.inf)
    m = jnp.max(s, axis=-1, keepdims=True)
    m = jnp.where(jnp.isfinite(m), m, 0.0)
    e = jnp.where(mask, jnp.exp(s - m), 0.0)
    return e / jnp.maximum(jnp.sum(e, axis=-1, keepdims=True), 1e-30)


def rope_tables(seq):
    pos = jnp.arange(seq, dtype=jnp.float32)
    inv = ROPE_THETA ** (-jnp.arange(0, ROPE_DIM, 2, dtype=jnp.float32) / ROPE_DIM)
    ang = pos[:, None] * inv[None, :]
    return jnp.cos(ang), jnp.sin(ang)


def partial_rope(x, cos, sin):
    half = ROPE_DIM // 2
    x1, x2, xp = x[..., :half], x[..., half:ROPE_DIM], x[..., ROPE_DIM:]
    c = cos.astype(x.dtype)
    s = sin.astype(x.dtype)
    return jnp.concatenate([x1 * c - x2 * s, x1 * s + x2 * c, xp], axis=-1)


def heads(a, n):
    B, T, W = a.shape
    return a.reshape(B, T, n, W // n).transpose(0, 2, 1, 3)


def hgrn2_chunkwise(q, log_f, k, v):
    B, H, T, DK = q.shape
    DV = v.shape[-1]
    C = HG_CHUNK
    N = T // C

    def to_chunks(a):
        return jnp.moveaxis(a.reshape(B, H, N, C, a.shape[-1]), 2, 0)

    causal = jnp.tril(jnp.ones((C, C), dtype=bool))[:, :, None]

    def step(S, inp):
        qi, lfi, ki, vi = inp
        b = jnp.cumsum(lfi.astype(jnp.float32), axis=-2)
        diff = b[..., :, None, :] - b[..., None, :, :]
        decay = jnp.exp(jnp.where(causal, diff, -jnp.inf))
        A = jnp.einsum('bhtd,bhsd,bhtsd->bhts', qi, ki, decay)
        o = jnp.einsum('bhts,bhsv->bhtv', A, vi) + jnp.einsum('bhtd,bhdv->bhtv', qi * jnp.exp(b), S)
        b_last = b[..., -1:, :]
        S_new = jnp.exp(b_last[..., 0, :])[..., None] * S + jnp.einsum(
            'bhsd,bhsv->bhdv', ki * jnp.exp(b_last - b), vi)
        return S_new, o

    S0 = jnp.zeros((B, H, DK, DV), jnp.float32)
    _, o = lax.scan(step, S0, (to_chunks(q), to_chunks(log_f), to_chunks(k), to_chunks(v)))
    return jnp.moveaxis(o, 0, 2).reshape(B, H, T, DV).astype(v.dtype)


def compress(kv, pe, w1, w2):
    B, G, T, dh = kv.shape
    NC = (T - CMP_BLOCK) // CMP_STRIDE + 1
    idx = jnp.arange(NC)[:, None] * CMP_STRIDE + jnp.arange(CMP_BLOCK)[None, :]
    blocks = kv[:, :, idx, :] + pe
    flat = blocks.reshape(B, G, NC, CMP_BLOCK * dh)
    return jax.nn.silu(flat @ w1) @ w2


def window_attn(q, k, v, scale):
    B, KVH, G, T, dh = q.shape
    NB = T // Q_BLOCK
    NW = WINDOW // Q_BLOCK
    pad = ((0, 0), (0, 0), (WINDOW, 0), (0, 0))
    kb = jnp.pad(k, pad).reshape(B, KVH, NB + NW, Q_BLOCK, dh)
    vb = jnp.pad(v, pad).reshape(B, KVH, NB + NW, Q_BLOCK, dh)
    band_k = jnp.concatenate([kb[:, :, i:i + NB] for i in range(NW + 1)], axis=3)
    band_v = jnp.concatenate([vb[:, :, i:i + NB] for i in range(NW + 1)], axis=3)
    qb = q.reshape(B, KVH, G, NB, Q_BLOCK, dh)
    s = jnp.einsum('bgnxqd,bgxkd->bgnxqk', qb, band_k) * scale
    qpos = jnp.arange(NB)[:, None] * Q_BLOCK + jnp.arange(Q_BLOCK)[None, :]
    kpos = jnp.arange(NB)[:, None] * Q_BLOCK - WINDOW + jnp.arange((NW + 1) * Q_BLOCK)[None, :]
    d = qpos[:, :, None] - kpos[:, None, :]
    mask = (d >= 0) & (d < WINDOW) & (kpos[:, None, :] >= 0)
    p = masked_softmax(s, mask)
    o = jnp.einsum('bgnxqk,bgxkd->bgnxqd', p.astype(v.dtype), band_v)
    return o.reshape(B, KVH, G, T, dh)


def selected_attn(q, k, v, sel, scale):
    B, KVH, G, T, dh = q.shape
    K = sel.shape[-1]
    NB = T // SLC_Q_BLOCK
    qb = jnp.moveaxis(q.reshape(B, KVH, G, NB, SLC_Q_BLOCK, dh), 3, 0)
    sb = jnp.moveaxis(sel.reshape(B, KVH, NB, SLC_Q_BLOCK, K), 2, 0)
    t0 = jnp.arange(NB) * SLC_Q_BLOCK
    bi = jnp.arange(B)[:, None, None, None]
    gi = jnp.arange(KVH)[None, :, None, None]

    def one(args):
        qx, sx, s0 = args
        tok = (sx[..., None] * SLC_BLOCK + jnp.arange(SLC_BLOCK)).reshape(B, KVH, SLC_Q_BLOCK, K * SLC_BLOCK)
        kg = k[bi, gi, tok]
        vg = v[bi, gi, tok]
        s = jnp.einsum('bgnqd,bgqkd->bgnqk', qx, kg) * scale
        qpos = s0 + jnp.arange(SLC_Q_BLOCK)
        mask = (tok <= qpos[:, None])[:, :, None]
        p = masked_softmax(s, mask)
        return jnp.einsum('bgnqk,bgqkd->bgnqd', p.astype(vg.dtype), vg)

    o = lax.map(one, (qb, sb, t0))
    return jnp.moveaxis(o, 0, 3).reshape(B, KVH, G, T, dh)


def setup_inputs(seed: int = 0) -> dict:
    key = jax.random.key(seed)
    ks = jax.random.split(key, 16)
    n = jax.random.normal
    f32 = jnp.float32
    return {
        "x": n(ks[0], (BATCH, SEQ, D_MODEL), f32),
        "p": n(ks[1], (DEPTH, BATCH, SEQ, PLE_DIM), f32),
        "norm_g": 1.0 + 0.05 * n(ks[2], (DEPTH, D_MODEL), f32),
        "w_in": n(ks[3], (DEPTH, D_MODEL, IN_COLS), f32) * D_MODEL ** -0.5,
        "hgrn_lb": 0.5 * n(ks[4], (DEPTH, HG_HEADS * HG_DK), f32),
        "hgrn_onorm_g": 1.0 + 0.05 * n(ks[5], (DEPTH, HG_DV), f32),
        "nsa_qnorm_g": 1.0 + 0.05 * n(ks[6], (DEPTH, NSA_DH), f32),
        "nsa_knorm_g": 1.0 + 0.05 * n(ks[7], (DEPTH, 3, NSA_DH), f32),
        "cmp_pe": 0.1 * n(ks[8], (DEPTH, 2, CMP_BLOCK, NSA_DH), f32),
        "cmp_w1": n(ks[9], (DEPTH, 2, CMP_BLOCK * NSA_DH, CMP_HIDDEN), f32) * (CMP_BLOCK * NSA_DH) ** -0.5,
        "cmp_w2": n(ks[10], (DEPTH, 2, CMP_HIDDEN, NSA_DH), f32) * CMP_HIDDEN ** -0.5,
        "w_out": n(ks[11], (DEPTH, D_MIX, D_MODEL), f32) * (0.5 * D_MIX ** -0.5),
        "ple_norm_g": 1.0 + 0.05 * n(ks[12], (DEPTH, D_MODEL), f32),
        "w_pg": n(ks[13], (DEPTH, D_MODEL, D_MODEL), f32) * D_MODEL ** -0.5,
        "w_pp": n(ks[14], (DEPTH, PLE_DIM, D_MODEL), f32) * (0.5 * PLE_DIM ** -0.5),
    }


def reference(x, p, norm_g, w_in, hgrn_lb, hgrn_onorm_g, nsa_qnorm_g, nsa_knorm_g,
              cmp_pe, cmp_w1, cmp_w2, w_out, ple_norm_g, w_pg, w_pp):
    B, T, _ = x.shape
    KVH, G, dh = NSA_KV_HEADS, NSA_G, NSA_DH
    scale = dh ** -0.5
    split_points = np.cumsum(IN_SPLITS)[:-1].tolist()

    cos, sin = rope_tables(T)
    lb_all = jnp.cumsum(jax.nn.softmax(hgrn_lb.astype(jnp.float32), axis=0), axis=0)
    lb_all = lb_all - lb_all[0]

    NC = (T - CMP_BLOCK) // CMP_STRIDE + 1
    NSB = T // SLC_BLOCK
    top_k = min(SLC_TOPK, NSB)
    c_tok = jnp.arange(NC)[:, None] * CMP_STRIDE + jnp.arange(CMP_BLOCK)[None, :]
    overlap = jnp.mean((c_tok[..., None] // SLC_BLOCK == jnp.arange(NSB)).astype(jnp.float32), axis=1)
    c_end = jnp.arange(NC) * CMP_STRIDE + CMP_BLOCK - 1
    cmp_mask = c_end[None, :] <= jnp.arange(T)[:, None]
    cur = (jnp.arange(T) // SLC_BLOCK)[:, None]
    jb = jnp.arange(NSB)[None, :]
    forced = (jb == 0) | (jb == cur) | (jb == cur - 1)
    eligible = jb <= cur

    h = x
    for i in range(DEPTH):
        xn = rmsnorm(h, norm_g[i])
        proj = xn @ w_in[i]
        (hq, hf, hi, hgo, hz, nq, kcm, vcm, ksl, vsl, kwn, vwn, ngate, nz) = jnp.split(proj, split_points, axis=-1)

        lb = lb_all[i].reshape(HG_HEADS, 1, HG_DK)
        fl = heads(hf, HG_HEADS).astype(jnp.float32)
        log_f = jnp.logaddexp(jnp.log(lb), jnp.log1p(-lb) + jax.nn.log_sigmoid(fl))
        k_hg = (1.0 - lb) * jax.nn.sigmoid(-fl)
        o_hg = hgrn2_chunkwise(heads(hq, HG_HEADS), log_f, k_hg, heads(hi, HG_HEADS))
        o_hg = rmsnorm(o_hg, hgrn_onorm_g[i]).transpose(0, 2, 1, 3).reshape(B, T, HG_W)
        y_hg = o_hg * jax.nn.sigmoid(hgo) * jax.nn.silu(hz)

        qn = rmsnorm(heads(nq, NSA_HEADS), nsa_qnorm_g[i])
        q_nope = qn.reshape(B, KVH, G, T, dh)
        q_rope = partial_rope(qn, cos, sin).reshape(B, KVH, G, T, dh)

        kc = rmsnorm(compress(heads(kcm, KVH), cmp_pe[i, 0], cmp_w1[i, 0], cmp_w2[i, 0]), nsa_knorm_g[i, 0])
        vc = compress(heads(vcm, KVH), cmp_pe[i, 1], cmp_w1[i, 1], cmp_w2[i, 1])
        s_cmp = jnp.einsum('bgntd,bgcd->bgntc', q_nope, kc) * scale
        p_cmp = masked_softmax(s_cmp, cmp_mask)
        o_cmp = jnp.einsum('bgntc,bgcd->bgntd', p_cmp.astype(vc.dtype), vc)

        imp = jnp.einsum('bgntc,cj->bgtj', p_cmp, overlap)
        score = jnp.where(forced, FORCE_SCORE, jnp.where(eligible, imp, -1.0))
        _, sel = lax.top_k(score, top_k)
        k_sl = partial_rope(rmsnorm(heads(ksl, KVH), nsa_knorm_g[i, 1]), cos, sin)
        o_slc = selected_attn(q_rope, k_sl, heads(vsl, KVH), sel, scale)

        k_wn = partial_rope(rmsnorm(heads(kwn, KVH), nsa_knorm_g[i, 2]), cos, sin)
        o_win = window_attn(q_rope, k_wn, heads(vwn, KVH), scale)

        gates = jax.nn.sigmoid(ngate).reshape(B, T, 3, KVH, G).transpose(2, 0, 3, 4, 1)[..., None]
        o_nsa = gates[0] * o_cmp + gates[1] * o_slc + gates[2] * o_win
        o_nsa = o_nsa.reshape(B, NSA_HEADS, T, dh).transpose(0, 2, 1, 3).reshape(B, T, NSA_W)
        y_nsa = o_nsa * jax.nn.silu(nz)

        h = h + jnp.concatenate([y_hg, y_nsa], axis=-1) @ w_out[i]

        gate = jax.nn.sigmoid(rmsnorm(h, ple_norm_g[i]) @ w_pg[i])
        h = h + gate * (p[i] @ w_pp[i])
    return h
```

```python
import contextlib
import numpy as np
import ml_dtypes
import concourse.bass as bass
import concourse.mybir as mybir
from concourse.bass_utils import run_bass_kernel_spmd

F32 = mybir.dt.float32
BF16 = mybir.dt.bfloat16
ALU = mybir.AluOpType
AF = mybir.ActivationFunctionType
AX = mybir.AxisListType

T = 4096
NT = 32
D = 1024
NIN = 2188
EPS = 1e-6
NEG = -30000.0

SAME_ENGINE_SYNC = True


class Res:
    __slots__ = ("name", "w", "r", "dsem", "excl")

    def __init__(self, name="", excl=False):
        self.name = name
        self.excl = excl
        self.w = None
        self.r = []
        self.dsem = None


class SemObj:
    def __init__(self, handle, name):
        self.h = handle
        self.name = name
        self.count = 0


class Eng:
    def __init__(self, name, handle_name):
        self.name = name
        self.handle_name = handle_name
        self.sem = None
        self.prog = []
        self.waited = {}
        self.is_pe = name == "pe"


class FW:
    def __init__(self, nc, stack):
        self.nc = nc
        self.stack = stack
        self.engs = {}
        for nm, hn in [("pe", "tensor"), ("dve", "vector"), ("act", "scalar"), ("pool", "gpsimd"), ("sp", "sync")]:
            e = Eng(nm, hn)
            e.sem = self.new_sem("s_" + nm)
            self.engs[nm] = e
        self.pe, self.dve, self.act, self.pool, self.sp = (self.engs[k] for k in ("pe", "dve", "act", "pool", "sp"))
        self.n_ops = 0
        self.nsem = 5

    def new_sem(self, name):
        h = self.stack.enter_context(self.nc.semaphore(name))
        return SemObj(h, name)

    def sb(self, name, shape, dtype):
        return self.stack.enter_context(self.nc.sbuf_tensor("sb_" + name, list(shape), dtype))

    def ps(self, name, shape, dtype):
        return self.stack.enter_context(self.nc.psum_tensor("ps_" + name, list(shape), dtype))

    def _collect(self, eng, reads, writes):
        toks = []
        for r in reads:
            if r.w is not None:
                toks.append(r.w)
        for w in writes:
            if w.w is not None:
                toks.append(w.w)
            toks.extend(w.r)
        need = {}
        for (s, v) in toks:
            if s is eng.sem and (eng.is_pe or not SAME_ENGINE_SYNC):
                continue
            if eng.waited.get(s, 0) >= v:
                continue
            if need.get(s, 0) < v:
                need[s] = v
        for s, v in need.items():
            eng.waited[s] = v
        return list(need.items())

    def _commit(self, tok, reads, writes):
        for r in reads:
            r.r = [t for t in r.r if t[0] is not tok[0]]
            r.r.append(tok)
        for w in writes:
            w.w = tok
            w.r = []

    def op(self, eng, fn, reads=(), writes=()):
        ex = [r for r in reads if r.excl]
        if ex:
            reads = [r for r in reads if not r.excl]
            writes = list(writes) + ex
        waits = self._collect(eng, reads, writes)
        eng.sem.count += 1
        tok = (eng.sem, eng.sem.count)
        eng.prog.append((waits, fn, (eng.sem, 1)))
        self._commit(tok, reads, writes)
        self.n_ops += 1
        return tok

    def dma(self, eng, out, in_, reads=(), writes=(), sem_res=None):
        if sem_res is None:
            sem_res = (list(writes) + list(reads))[0]
        if sem_res.dsem is None:
            sem_res.dsem = self.new_sem("d%d" % self.nsem)
            self.nsem += 1
        S = sem_res.dsem
        waits = self._collect(eng, reads, writes)
        if S.count > 0 and eng.waited.get(S, 0) < S.count:
            waits = [(s, v) for (s, v) in waits if s is not S] + [(S, S.count)]
            eng.waited[S] = S.count
        S.count += 16
        tok = (S, S.count)
        eng.prog.append((waits, (lambda e, o=out, i=in_: e.dma_start(out=o, in_=i)), (S, 16)))
        self._commit(tok, reads, writes)
        self.n_ops += 1
        return tok

    def wait_tokens(self, eng, toks):
        eng.prog.append(([(s, v) for (s, v) in toks], None, None))

    def emit(self):
        nc = self.nc
        with nc.Block() as block:
            for nm, e in self.engs.items():
                if not e.prog:
                    continue
                deco = getattr(block, e.handle_name)

                def body(h, e=e):
                    for waits, fn, inc in e.prog:
                        for (s, v) in waits:
                            h.wait_ge(s.h, v)
                        if fn is not None:
                            ins = fn(h)
                            ins.then_inc(inc[0].h, inc[1])
                deco(body)


CF_MASKHG = 0
CF_VCMP = 128
CF_ETMPL = 640
CF_CTMPL = 768
CF_OV = 896
CF_ROPE = 1024
CF_ONES = 1536
CF_NEGH = 1664
CF_N = 1680
CB_ID = 0
CB_CAUS = 128
CB_WIN = 640
CB_RBIG = 1152
CB_N = 5248


def _const_tables():
    cf = np.zeros((128, CF_N), np.float32)
    s = np.arange(128)[:, None]
    t = np.arange(128)[None, :]
    cf[:, CF_MASKHG:CF_MASKHG + 128] = ((s <= t) & ((s // 64) == (t // 64))).astype(np.float32)
    v = (16 * s + 31 - t).astype(np.float32)
    cf[:, CF_VCMP:CF_VCMP + 512] = np.tile(v, (1, 4))
    tl = np.arange(128)[:, None]
    m = np.arange(126)[None, :]
    hi = (tl >= 64).astype(np.int64)
    u = m - 62
    E = (u <= hi).astype(np.float32)
    Fs = ((u == hi) | (u == hi - 1)).astype(np.float32)
    cf[:, CF_ETMPL:CF_ETMPL + 126] = E
    cf[:, CF_CTMPL:CF_CTMPL + 126] = E - 1.0 + 1e4 * Fs
    c = np.arange(255)
    ctok = c[:, None] * 16 + np.arange(32)[None, :]
    ov = np.mean((ctok[..., None] // 64 == np.arange(64)).astype(np.float32), axis=1)
    ovp = np.zeros((256, 64), np.float32)
    ovp[:255] = ov
    cf[:, CF_OV:CF_OV + 64] = ovp[0:128]
    cf[:, CF_OV + 64:CF_OV + 128] = ovp[128:256]
    pos = np.arange(T, dtype=np.float32)
    inv = (np.float32(500000.0) ** (-np.arange(0, 16, 2, dtype=np.float32) / np.float32(16))).astype(np.float32)
    ang = (pos[:, None] * inv[None, :]).astype(np.float32)
    cs = np.concatenate([np.cos(ang), np.sin(ang)], axis=1).astype(np.float32)
    cf[:, CF_ROPE:CF_ROPE + 512] = cs.reshape(NT, 128, 16).transpose(1, 0, 2).reshape(128, 512)
    cf[:, CF_ONES:CF_ONES + 128] = 1.0
    cf[:, CF_NEGH:CF_NEGH + 8] = -0.5
    cb = np.zeros((128, CB_N), np.float32)
    cb[:, CB_ID:CB_ID + 128] = np.eye(128)
    caus = np.where(s <= t, 0.0, NEG)
    cb[:, CB_CAUS:CB_CAUS + 512] = np.tile(caus, (1, 4))
    win = np.where(s > t, 0.0, NEG)
    cb[:, CB_WIN:CB_WIN + 512] = np.tile(win, (1, 4))
    j = np.arange(64)[:, None]
    sg = np.arange(4096)[None, :]
    cb[0:64, CB_RBIG:CB_RBIG + 4096] = (sg // 64 == j).astype(np.float32)
    return cf, cb.astype(ml_dtypes.bfloat16)


def _in_cols(g):
    off = {"hq": 0, "hf": 512, "hi": 1024, "hgo": 1536, "hz": 2048, "nq": 2560, "kcm": 3072, "vcm": 3200,
           "ksl": 3328, "vsl": 3456, "kwn": 3584, "vwn": 3712, "ngate": 3840, "nz": 3864}
    r = lambda a, n: list(range(a, a + n))
    cols = []
    cols += r(off["hi"] + 256 * g, 256) + r(off["hgo"] + 256 * g, 256)
    cols += r(off["hz"] + 256 * g, 256) + r(off["nz"] + 256 * g, 256)
    cols += r(off["nq"] + 256 * g, 256) + r(off["ksl"] + 64 * g, 64) + r(off["kwn"] + 64 * g, 64) \
        + r(off["vsl"] + 64 * g, 64) + r(off["vwn"] + 64 * g, 64)
    cols += [off["ngate"] + br * 8 + 4 * g + n for br in range(3) for n in range(4)]
    cols += r(off["hq"] + 256 * g, 256) + r(off["hf"] + 256 * g, 256)
    cols += r(off["kcm"] + 64 * g, 64) + r(off["vcm"] + 64 * g, 64)
    assert len(cols) == NIN
    return np.array(cols)


_WOUT_ROWS = np.concatenate([np.arange(0, 128), np.arange(128, 256), np.arange(512, 640), np.arange(640, 768),
                             np.arange(256, 384), np.arange(384, 512), np.arange(768, 896), np.arange(896, 1024)])


def build_pass(doB, doA):
    nc = bass.Bass("TRN2", target_bir_lowering=False)

    def din(name, shape, dtype=F32):
        return nc.dram_tensor(name, list(shape), dtype, kind="ExternalInput").ap()

    def dout(name, shape, dtype=F32):
        return nc.dram_tensor(name, list(shape), dtype, kind="ExternalOutput").ap()

    def dint(name, shape, dtype):
        return nc.dram_tensor(name, list(shape), dtype, kind="Internal").ap()

    h_in = din("h_in", [T, D])
    cf_d = din("cf", [128, CF_N])
    cb_d = din("cb", [128, CB_N], BF16)
    if doB:
        yT_in = din("yT_in", [8, 128, T], BF16)
        p_in = din("p_in", [T, 256])
        wout_d = din("wout", [1024, 1024])
        wpg_d = din("wpg", [1024, 1024])
        wpp_d = din("wpp", [256, 1024])
        bvec_d = din("bvec", [128, 16])
        h_out = dout("h_out", [T, D])
    if doA:
        win_d = din("win", [1024, NIN])
        avec_d = din("avec", [128, 32])
        gqk_d = din("gqk", [128, 7 * 64])
        w1_d = din("w1", [2, 2048, 128])
        w2_d = din("w2", [128, 2, 64])
        peT_d = din("peT", [64, 2, 32])
        yT_out = dout("yT_out", [4, 128, T], BF16)
        QT_d = dint("QT_scr", [NT, 128, 512], BF16)
        NZ_d = dint("NZ_scr", [NT, 128, 256], BF16)

    with contextlib.ExitStack() as st:
        fw = FW(nc, st)
        pe, dve, act, pool, sp = fw.pe, fw.dve, fw.act, fw.pool, fw.sp

        def mm(out, lhsT, rhs, start, stop, reads, writes):
            fw.op(pe, lambda e: e.matmul(out, lhsT, rhs, start=start, stop=stop, skip_group_check=True), reads, writes)

        def tr(out, in_, reads, writes):
            k_ = in_.shape[0]
            idn = cb[0:k_, CB_ID:CB_ID + k_]
            fw.op(pe, lambda e: e.transpose(out, in_, idn), list(reads) + [r_cb], writes)

        def actf(out, in_, func, reads, writes, bias=None, scale=None, accum=None, eng=None):
            kw = {}
            if bias is not None:
                kw["bias"] = bias
            if scale is not None:
                kw["scale"] = scale
            if accum is not None:
                kw["accum_out"] = accum
            fw.op(act, lambda e: e.activation(out=out, in_=in_, func=func, **kw), reads, writes)

        def tt(eng, out, a, b, op, reads, writes):
            fw.op(eng, lambda e: e.tensor_tensor(out, a, b, op), reads, writes)

        def ts(eng, out, a, s1, s2, op0, op1, reads, writes):
            if op1 is None:
                fw.op(eng, lambda e: e.tensor_scalar(out, a, s1, None, op0), reads, writes)
            else:
                fw.op(eng, lambda e: e.tensor_scalar(out, a, s1, s2, op0, op1), reads, writes)

        def stt(out, in0, scalar, in1, op0, op1, reads, writes):
            fw.op(dve, lambda e: e.scalar_tensor_tensor(out, in0, scalar, in1, op0, op1), reads, writes)

        def cp(eng, out, in_, reads, writes):
            if eng is act:
                fw.op(act, lambda e: e.activation(out=out, in_=in_, func=AF.Copy), reads, writes)
            else:
                fw.op(eng, lambda e: e.tensor_copy(out, in_), reads, writes)

        cf = fw.sb("cf", [128, CF_N], F32); r_cf = Res("cf")
        cb = fw.sb("cb", [128, CB_RBIG], BF16); r_cb = Res("cb")
        fw.dma(sp, cf[:], cf_d[:, :], writes=[r_cf])
        fw.dma(sp, cb[:], cb_d[:, 0:CB_RBIG], writes=[r_cb])
        ident = cb[:, CB_ID:CB_ID + 128]
        negh = cf[:, CF_NEGH:CF_NEGH + 8]
        ones = cf[:, CF_ONES:CF_ONES + 128]

        PB = [fw.ps("pb%d" % i, [128, 512], F32) for i in range(6)]
        rPB = [Res("pb%d" % i, True) for i in range(6)]
        PT = [fw.ps("pt%d" % i, [128, 1024], BF16) for i in range(2)]
        rPT = [Res("pt%d" % i, True) for i in range(2)]

        h_sb = [fw.sb("h%d" % i, [128, D], F32) for i in range(2)]
        r_h = [Res("h%d" % i) for i in range(2)]
        ss = fw.sb("ss", [128, 8], F32); r_ss = Res("ss")
        rstd = fw.sb("rstd", [128, 8], F32); r_rstd = Res("rstd")
        xn = fw.sb("xn", [128, D], BF16); r_xn = Res("xn")
        xnT = fw.sb("xnT", [128, D], BF16); r_xnT = Res("xnT")
        stage = [fw.sb("stage%d" % i, [128, 1094], F32) for i in range(2)]
        r_stage = [Res("stage%d" % i) for i in range(2)]
        stage_i = [0]

        def load_convert(dst_ap, src_ap, ncols, scale_ap, r_dst, extra_reads=(), npart=128):
            i = stage_i[0] % 2
            stage_i[0] += 1
            fw.dma(sp, stage[i][0:npart, 0:ncols], src_ap, writes=[r_stage[i]])
            if scale_ap is None:
                cp(pool, dst_ap, stage[i][0:npart, 0:ncols], [r_stage[i]], [r_dst])
            else:
                ts(pool, dst_ap, stage[i][0:npart, 0:ncols], scale_ap, None, ALU.mult, None,
                   [r_stage[i]] + list(extra_reads), [r_dst])

        def rmsnorm_T(hb, r_hb):
            actf(xn[:], hb[:], AF.Square, [r_hb], [r_xn, r_ss], accum=ss[:, 0:1])
            ts(pool, ss[:, 1:2], ss[:, 0:1], 1.0 / D, EPS, ALU.mult, ALU.add, [r_ss], [r_ss])
            tt(pool, rstd[:, 0:1], ss[:, 1:2], negh[:, 0:1], ALU.pow, [r_ss, r_cf], [r_rstd])
            ts(dve, xn[:], hb[:], rstd[:, 0:1], None, ALU.mult, None, [r_hb, r_rstd], [r_xn])
            for kc in range(8):
                tr(PT[0][:, kc * 128:(kc + 1) * 128], xn[:, kc * 128:(kc + 1) * 128], [r_xn], [rPT[0]])
            cp(dve, xnT[:, 0:512], PT[0][:, 0:512], [rPT[0]], [r_xnT])
            cp(act, xnT[:, 512:1024], PT[0][:, 512:1024], [rPT[0]], [r_xnT])

        if doB:
            WB = fw.sb("WB", [128, 18, 1024], BF16); r_WB = Res("WB")
            bvec = fw.sb("bvec", [128, 16], F32); r_bvec = Res("bvec")
            fw.dma(sp, bvec[:], bvec_d[:, :], writes=[r_bvec])
            yT_sb = [fw.sb("yTs%d" % i, [128, 8, 128], BF16) for i in range(2)]
            r_yT = [Res("yTs%d" % i) for i in range(2)]
            p_sb = [fw.sb("ps%d" % i, [128, 256], F32) for i in range(2)]
            r_p = [Res("ps%d" % i) for i in range(2)]
            pb16 = fw.sb("pb16", [128, 256], BF16); r_pb16 = Res("pb16")
            pT = fw.sb("pT", [128, 256], BF16); r_pT = Res("pT")
            gate = fw.sb("gate", [128, 512], F32); r_gate = Res("gate")

            def load_WB():
                for kc in range(8):
                    sc = bvec[:, 0:1] if (kc % 4) < 2 else None
                    load_convert(WB[:, kc, :], wout_d[kc * 128:(kc + 1) * 128, :], 1024, sc, r_WB, [r_bvec])
                for kc in range(8):
                    load_convert(WB[:, 8 + kc, :], wpg_d[kc * 128:(kc + 1) * 128, :], 1024, bvec[:, 8 + kc:9 + kc], r_WB, [r_bvec])
                for kc in range(2):
                    load_convert(WB[:, 16 + kc, :], wpp_d[kc * 128:(kc + 1) * 128, :], 1024, None, r_WB)

            def B_load(qt):
                i = qt % 2
                fw.dma(sp, yT_sb[i][:], yT_in[:, :, qt * 128:(qt + 1) * 128].rearrange("c p t -> p c t"), writes=[r_yT[i]])
                fw.dma(sp, p_sb[i][:], p_in[qt * 128:(qt + 1) * 128, :], writes=[r_p[i]])

            def B_tile(qt):
                i = qt % 2
                hb, rh = h_sb[i], r_h[i]
                for cg in range(2):
                    for kc in range(8):
                        mm(PB[cg][:, :], yT_sb[i][:, kc, :], WB[:, kc, cg * 512:(cg + 1) * 512], kc == 0, kc == 7,
                           [r_yT[i], r_WB], [rPB[cg]])
                for cg in range(2):
                    tt(dve, hb[:, cg * 512:(cg + 1) * 512], hb[:, cg * 512:(cg + 1) * 512], PB[cg][:, :], ALU.add,
                       [rh, rPB[cg]], [rh])
                rmsnorm_T(hb, rh)
                for cg in range(2):
                    for kc in range(8):
                        mm(PB[2 + cg][:, :], xnT[:, kc * 128:(kc + 1) * 128], WB[:, 8 + kc, cg * 512:(cg + 1) * 512],
                           kc == 0, kc == 7, [r_xnT, r_WB], [rPB[2 + cg]])
                cp(pool, pb16[:], p_sb[i][:], [r_p[i]], [r_pb16])
                for k2 in range(2):
                    tr(PT[1][:, k2 * 128:(k2 + 1) * 128], pb16[:, k2 * 128:(k2 + 1) * 128], [r_pb16], [rPT[1]])
                cp(dve, pT[:], PT[1][:, 0:256], [rPT[1]], [r_pT])
                for cg in range(2):
                    for k2 in range(2):
                        mm(PB[4 + cg][:, :], pT[:, k2 * 128:(k2 + 1) * 128], WB[:, 16 + k2, cg * 512:(cg + 1) * 512],
                           k2 == 0, k2 == 1, [r_pT, r_WB], [rPB[4 + cg]])
                for cg in range(2):
                    actf(gate[:], PB[2 + cg][:, :], AF.Sigmoid, [rPB[2 + cg]], [r_gate])
                    tt(dve, gate[:], gate[:], PB[4 + cg][:, :], ALU.mult, [r_gate, rPB[4 + cg]], [r_gate])
                    tt(pool, hb[:, cg * 512:(cg + 1) * 512], hb[:, cg * 512:(cg + 1) * 512], gate[:], ALU.add, [rh, r_gate], [rh])
                fw.dma(sp, h_out[qt * 128:(qt + 1) * 128, :], hb[:], reads=[rh], sem_res=rh)

        if doA:
            WX = fw.sb("WX", [128, 8, NIN], BF16); r_WX = Res("WX")
            avec = fw.sb("avec", [128, 32], F32); r_avec = Res("avec")
            fw.dma(sp, avec[:], avec_d[:, :], writes=[r_avec])
            gqk = fw.sb("gqk", [128, 7, 64], F32); r_gqk = Res("gqk")
            fw.dma(sp, gqk[:].rearrange("p a b -> p (a b)"), gqk_d[:, :], writes=[r_gqk])
            ts(pool, gqk[:, 0:4, :], gqk[:, 0:4, :], 0.125, None, ALU.mult, None, [r_gqk], [r_gqk])
            W1 = fw.sb("W1", [64, 2, 32, 128], BF16); r_W1 = Res("W1")
            W2 = fw.sb("W2", [128, 2, 64], BF16); r_W2 = Res("W2")
            peT = fw.sb("peT", [64, 2, 32], BF16); r_peT = Res("peT")

            def load_WA():
                for kc in range(8):
                    for hf_ in range(2):
                        load_convert(WX[:, kc, hf_ * 1094:(hf_ + 1) * 1094], win_d[kc * 128:(kc + 1) * 128, hf_ * 1094:(hf_ + 1) * 1094],
                                     1094, avec[:, kc:kc + 1], r_WX, [r_avec])

            def load_cmp_w():
                for x in range(2):
                    for lg in range(4):
                        src = w1_d[x, lg * 512:(lg + 1) * 512, :].rearrange("(l d) j -> d l j", d=64)
                        i = stage_i[0] % 2
                        stage_i[0] += 1
                        fw.dma(sp, stage[i][0:64, 0:1024].rearrange("p (l j) -> p l j", j=128), src, writes=[r_stage[i]])
                        cp(pool, W1[:, x, lg * 8:(lg + 1) * 8, :], stage[i][0:64, 0:1024].rearrange("p (l j) -> p l j", j=128),
                           [r_stage[i]], [r_W1])
                i = stage_i[0] % 2
                stage_i[0] += 1
                fw.dma(sp, stage[i][:, 0:128], w2_d[:, :, :].rearrange("p a b -> p (a b)"), writes=[r_stage[i]])
                cp(pool, W2[:].rearrange("p a b -> p (a b)"), stage[i][:, 0:128], [r_stage[i]], [r_W2])
                i = stage_i[0] % 2
                stage_i[0] += 1
                fw.dma(sp, stage[i][0:64, 0:64], peT_d[:, :, :].rearrange("p a b -> p (a b)"), writes=[r_stage[i]])
                cp(pool, peT[:].rearrange("p a b -> p (a b)"), stage[i][0:64, 0:64], [r_stage[i]], [r_peT])

            lbw = fw.sb("lbw", [128, 32], F32); r_lbw = Res("lbw")
            actf(lbw[:, 0:8], avec[:, 8:16], AF.Exp, [r_avec], [r_lbw])
            fw.op(dve, lambda e: e.tensor_reduce(out=lbw[:, 8:10], in_=lbw[:, 0:8].rearrange("p (h l) -> p h l", l=4),
                                                 op=ALU.add, axis=AX.X), [r_lbw], [r_lbw])
            fw.op(dve, lambda e: e.reciprocal(lbw[:, 10:12], lbw[:, 8:10]), [r_lbw], [r_lbw])
            tt(dve, lbw[:, 12:20].rearrange("p (h l) -> p h l", l=4), lbw[:, 0:8].rearrange("p (h l) -> p h l", l=4),
               avec[:, 16:20].unsqueeze(1).broadcast_to([128, 2, 4]), ALU.mult, [r_lbw, r_avec], [r_lbw])
            fw.op(dve, lambda e: e.tensor_reduce(out=lbw[:, 20:22], in_=lbw[:, 12:20].rearrange("p (h l) -> p h l", l=4),
                                                 op=ALU.add, axis=AX.X), [r_lbw], [r_lbw])
            tt(dve, lbw[:, 22:24], lbw[:, 20:22], lbw[:, 10:12], ALU.mult, [r_lbw], [r_lbw])
            ts(dve, lbw[:, 24:26], lbw[:, 22:24], -1.0, 1.0, ALU.mult, ALU.add, [r_lbw], [r_lbw])
            ts(dve, lbw[:, 26:28], lbw[:, 22:24], 1.0, -1.0, ALU.mult, ALU.add, [r_lbw], [r_lbw])

            KT = fw.sb("KT", [128, 2, T], BF16)
            r_KT = [Res("KT%d" % i) for i in range(NT)]
            r_KTc = Res("KTc")
            fw.dma(sp, KT[0:64, 0, :], cb_d[0:64, CB_RBIG:CB_RBIG + T], writes=[r_KTc])
            fw.op(pool, lambda e: e.memset(KT[0:64, 1, :], 0.0), [], [r_KTc])
            KVC = fw.sb("KVC", [64, 2, T], BF16)
            r_KVC = [Res("KVC%d" % i) for i in range(NT)]
            V_all = fw.sb("Vall", [128, NT, 2, 65], BF16)
            r_V = [Res("V%d" % i) for i in range(NT)]
            r_Vones = Res("Vones")
            fw.op(pool, lambda e: e.memset(V_all[:, :, :, 64:65], 1.0), [], [r_Vones])
            GATES = fw.sb("GATES", [128, NT, 12], F32)
            r_G = [Res("G%d" % i) for i in range(NT)]
            KCT = fw.sb("KCT", [64, 256], BF16); r_KCT = Res("KCT")
            VCX = fw.sb("VCX", [128, 2, 129], BF16); r_VCX = Res("VCX")

            v_sb = fw.sb("v_sb", [128, 256], BF16); r_v = Res("v_sb")
            g1 = fw.sb("g1", [128, 256], F32); r_g1 = Res("g1")
            g2 = fw.sb("g2", [128, 512], BF16); r_g2 = Res("g2")
            gateA = fw.sb("gateA", [128, 256], F32); r_gateA = Res("gateA")
            sq = fw.sb("sq", [128, 384], F32); r_sq = Res("sq")
            ss6 = fw.sb("ss6", [128, 16], F32); r_ss6 = Res("ss6")
            qk = fw.sb("qk", [128, 6, 64], F32); r_qk = Res("qk")
            rot = fw.sb("rot", [128, 6, 16], F32); r_rot = Res("rot")
            rtmp = fw.sb("rtmp", [128, 4, 6, 8], F32); r_rtmp = Res("rtmp")
            qcat = fw.sb("qcat", [128, 4, 128], BF16); r_qcat = Res("qcat")
            kcat = fw.sb("kcat", [128, 2, 64], BF16); r_kcat = Res("kcat")
            QTt = fw.sb("QTt", [128, 512], BF16); r_QTt = Res("QTt")
            yhg = fw.sb("yhg", [128, 256], BF16); r_yhg = Res("yhg")
            yTt = fw.sb("yTt", [128, 2, 128], BF16); r_yTt = Res("yTt")
            sg = fw.sb("sg", [128, 128], F32); r_sg = Res("sg")
            ff = fw.sb("ff", [128, 128], F32); r_ff = Res("ff")
            gT = fw.sb("gT", [128, 128], F32); r_gT = Res("gT")
            Bc = fw.sb("Bc", [128, 128], F32); r_Bc = Res("Bc")
            kT = fw.sb("kT", [128, 128], F32); r_kT = Res("kT")
            E1 = fw.sb("E1", [128, 128], F32); r_E1 = Res("E1")
            E1i = fw.sb("E1i", [128, 128], F32); r_E1i = Res("E1i")
            qsT = fw.sb("qsT", [128, 128], BF16); r_qsT = Res("qsT")
            ksT = fw.sb("ksT", [128, 128], BF16); r_ksT = Res("ksT")
            ks_sb = fw.sb("ks_sb", [128, 128], BF16); r_ks = Res("ks_sb")
            AT_sb = fw.sb("AT_sb", [128, 128], BF16); r_AT = Res("AT_sb")
            scl = fw.sb("scl", [128, 16], F32); r_scl = Res("scl")
            zz = fw.sb("zz", [128, 4], F32); r_zz = Res("zz")
            fw.op(pool, lambda e: e.memset(zz[:], 0.0), [], [r_zz])
            S_st = [fw.sb("S%d" % i, [128, 128], F32) for i in range(2)]
            r_S = [Res("S%d" % i) for i in range(2)]
            Sp = [fw.sb("Sp%d" % i, [128, 128], BF16) for i in range(2)]
            r_Sp = [Res("Sp%d" % i) for i in range(2)]
            tmpX = fw.sb("tmpX", [128, 128], F32); r_tmpX = Res("tmpX")
            oss = fw.sb("oss", [128, 8], F32); r_oss = Res("oss")
            ojunk = fw.sb("ojunk", [128, 128], BF16); r_ojunk = Res("ojunk")
            for hh in range(2):
                fw.op(pool, lambda e, hh=hh: e.memset(S_st[hh][:], 0.0), [], [r_S[hh]])
            nz_st = fw.sb("nz_st", [128, 256], BF16); r_nzst = Res("nz_st")

            maskHG = cf[:, CF_MASKHG:CF_MASKHG + 128]

            def A1_tile(qt, hb, rh):
                rmsnorm_T(hb, rh)
                for G in range(3):
                    for kc in range(8):
                        mm(PB[G][:, :], xnT[:, kc * 128:(kc + 1) * 128], WX[:, kc, G * 512:(G + 1) * 512], kc == 0, kc == 7,
                           [r_xnT, r_WX], [rPB[G]])
                for f in range(4):
                    for kc in range(8):
                        mm(PB[3][:, f * 128:(f + 1) * 128], WX[:, kc, 1548 + f * 128:1548 + (f + 1) * 128],
                           xnT[:, kc * 128:(kc + 1) * 128], kc == 0, kc == 7, [r_xnT, r_WX], [rPB[3]])
                for f in range(2):
                    for kc in range(8):
                        mm(PB[4][0:64, f * 128:(f + 1) * 128], WX[:, kc, 2060 + f * 64:2060 + (f + 1) * 64],
                           xnT[:, kc * 128:(kc + 1) * 128], kc == 0, kc == 7, [r_xnT, r_WX], [rPB[4]])
                for kc in range(8):
                    mm(PB[4][:, 256:268], xnT[:, kc * 128:(kc + 1) * 128], WX[:, kc, 1536:1548], kc == 0, kc == 7,
                       [r_xnT, r_WX], [rPB[4]])
                cp(dve, v_sb[:], PB[0][:, 0:256], [rPB[0]], [r_v])
                actf(g1[:], PB[0][:, 256:512], AF.Sigmoid, [rPB[0]], [r_g1])
                actf(g2[:], PB[1][:, :], AF.Silu, [rPB[1]], [r_g2])
                tt(dve, gateA[:], g1[:], g2[:, 0:256], ALU.mult, [r_g1, r_g2], [r_gateA])
                cp(pool, nz_st[:], g2[:, 256:512], [r_g2], [r_nzst])
                fw.dma(sp, NZ_d[qt, :, :], nz_st[:], reads=[r_nzst], writes=[r_NZd[qt]], sem_res=r_nzst)
                actf(GATES[:, qt, :], PB[4][:, 256:268], AF.Sigmoid, [rPB[4]], [r_G[qt]])
                cp(act, KVC[:, :, qt * 128:(qt + 1) * 128], PB[4][0:64, 0:256].rearrange("p (a t) -> p a t", t=128),
                   [rPB[4]], [r_KVC[qt]])
                cp(act, V_all[:, qt, :, 0:64], PB[2][:, 384:512].rearrange("p (a d) -> p a d", d=64), [rPB[2]], [r_V[qt]])
                actf(sq[:], PB[2][:, 0:384], AF.Square, [rPB[2]], [r_sq])
                fw.op(dve, lambda e: e.tensor_reduce(out=ss6[:, 0:6], in_=sq[:].rearrange("p (a d) -> p a d", d=64),
                                                     op=ALU.add, axis=AX.X), [r_sq], [r_ss6])
                ts(pool, ss6[:, 8:14], ss6[:, 0:6], 1.0 / 64, EPS, ALU.mult, ALU.add, [r_ss6], [r_ss6])
                tt(pool, ss6[:, 0:6], ss6[:, 8:14], negh[:, 0:6], ALU.pow, [r_ss6, r_cf], [r_ss6])
                tt(dve, qk[:], PB[2][:, 0:384].rearrange("p (a d) -> p a d", d=64),
                   ss6[:, 0:6].unsqueeze(2).broadcast_to([128, 6, 64]), ALU.mult, [rPB[2], r_ss6], [r_qk])
                tt(pool, qk[:], qk[:], gqk[:, 0:6, :], ALU.mult, [r_qk, r_gqk], [r_qk])
                cs = cf[:, CF_ROPE + qt * 16:CF_ROPE + qt * 16 + 16]
                cosb = cs[:, 0:8].unsqueeze(1).broadcast_to([128, 6, 8])
                sinb = cs[:, 8:16].unsqueeze(1).broadcast_to([128, 6, 8])
                x1 = qk[:, :, 0:8]
                x2 = qk[:, :, 8:16]
                tt(dve, rtmp[:, 0, :, :], x1, cosb, ALU.mult, [r_qk, r_cf], [r_rtmp])
                tt(dve, rtmp[:, 1, :, :], x2, sinb, ALU.mult, [r_qk, r_cf], [r_rtmp])
                tt(pool, rtmp[:, 2, :, :], x1, sinb, ALU.mult, [r_qk, r_cf], [r_rtmp])
                tt(pool, rtmp[:, 3, :, :], x2, cosb, ALU.mult, [r_qk, r_cf], [r_rtmp])
                tt(dve, rot[:, :, 0:8], rtmp[:, 0, :, :], rtmp[:, 1, :, :], ALU.subtract, [r_rtmp], [r_rot])
                tt(dve, rot[:, :, 8:16], rtmp[:, 2, :, :], rtmp[:, 3, :, :], ALU.add, [r_rtmp], [r_rot])
                cp(dve, qcat[:, :, 0:64], qk[:, 0:4, :], [r_qk], [r_qcat])
                cp(pool, qcat[:, :, 80:128], qk[:, 0:4, 16:64], [r_qk], [r_qcat])
                cp(dve, qcat[:, :, 64:80], rot[:, 0:4, :], [r_rot], [r_qcat])
                cp(pool, kcat[:, :, 16:64], qk[:, 4:6, 16:64], [r_qk], [r_kcat])
                cp(dve, kcat[:, :, 0:16], rot[:, 4:6, :], [r_rot], [r_kcat])
                for n in range(4):
                    tr(PT[1][:, n * 128:(n + 1) * 128], qcat[:, n, :], [r_qcat], [rPT[1]])
                for a in range(2):
                    tr(PT[1][64:128, 512 + a * 128:512 + (a + 1) * 128], kcat[:, a, :], [r_kcat], [rPT[1]])
                cp(dve, QTt[:], PT[1][:, 0:512], [rPT[1]], [r_QTt])
                cp(act, KT[64:128, :, qt * 128:(qt + 1) * 128], PT[1][64:128, 512:768].rearrange("p (a t) -> p a t", t=128),
                   [rPT[1]], [r_KT[qt]])
                fw.dma(sp, QT_d[qt, :, :], QTt[:], reads=[r_QTt], writes=[r_QTd[qt]], sem_res=r_QTt)
                for hh in range(2):
                    qTp = PB[3][:, hh * 128:(hh + 1) * 128]
                    flT = PB[3][:, (2 + hh) * 128:(3 + hh) * 128]
                    lb_c = lbw[:, 22 + hh:23 + hh]
                    oml_c = lbw[:, 24 + hh:25 + hh]
                    noml_c = lbw[:, 26 + hh:27 + hh]
                    actf(sg[:], flT, AF.Sigmoid, [rPB[3]], [r_sg])
                    ts(dve, ff[:], sg[:], oml_c, lb_c, ALU.mult, ALU.add, [r_sg, r_lbw], [r_ff])
                    actf(gT[:], ff[:], AF.Ln, [r_ff], [r_gT])
                    fw.op(dve, lambda e: e.tensor_tensor_scan(Bc[:], ones, gT[:], 0.0, ALU.mult, ALU.add),
                          [r_gT, r_cf], [r_Bc])
                    ts(pool, kT[:], sg[:], noml_c, oml_c, ALU.mult, ALU.add, [r_sg, r_lbw], [r_kT])
                    cp(pool, zz[:, 2:4], Bc[:, 63:64].broadcast_to([128, 2]), [r_Bc], [r_zz])
                    tt(pool, scl[:, 0:4], Bc[:, 31:128:32], zz[:, 0:4], ALU.subtract, [r_Bc, r_zz], [r_scl])
                    tt(pool, scl[:, 4:6], Bc[:, 63:128:64], Bc[:, 31:128:64], ALU.subtract, [r_Bc], [r_scl])
                    ts(pool, scl[:, 6:8], Bc[:, 31:128:64], -1.0, None, ALU.mult, None, [r_Bc], [r_scl])
                    actf(scl[:, 8:14], scl[:, 0:6], AF.Exp, [r_scl], [r_scl])
                    for c in range(2):
                        cs_ = slice(64 * c, 64 * c + 64)
                        actf(E1[:, cs_], Bc[:, cs_], AF.Exp, [r_Bc, r_scl], [r_E1], bias=scl[:, 6 + c:7 + c])
                        actf(E1i[:, cs_], Bc[:, cs_], AF.Exp, [r_Bc], [r_E1i], bias=Bc[:, 31 + 64 * c:32 + 64 * c], scale=-1.0)
                    tt(dve, qsT[:], qTp, E1[:], ALU.mult, [rPB[3], r_E1], [r_qsT])
                    tt(pool, ksT[:], kT[:], E1i[:], ALU.mult, [r_kT, r_E1i], [r_ksT])
                    tr(PT[0][:, 0:128], ksT[:], [r_ksT], [rPT[0]])
                    cp(act, ks_sb[:], PT[0][:, 0:128], [rPT[0]], [r_ks])
                    mm(PB[5][:, 0:128], ksT[:], qsT[:], True, True, [r_ksT, r_qsT], [rPB5q[0]])
                    tt(dve, AT_sb[:], PB[5][:, 0:128], maskHG, ALU.mult, [rPB5q[0], r_cf], [r_AT])
                    vh = v_sb[:, hh * 128:(hh + 1) * 128]
                    mm(PB[5][:, 256:384], ks_sb[0:64, :], v_sb[0:64, hh * 128:(hh + 1) * 128], True, True, [r_ks, r_v], [rPB[5]])
                    mm(PB[4][:, 384:512], ks_sb[64:128, :], v_sb[64:128, hh * 128:(hh + 1) * 128], True, True, [r_ks, r_v], [rPB[4]])
                    S = S_st[hh]
                    rS = r_S[hh]
                    actf(Sp[0][:], S[:], AF.Identity, [rS, r_scl], [r_Sp[0]], scale=scl[:, 8:9])
                    ts(dve, tmpX[:], PB[5][:, 256:384], scl[:, 12:13], None, ALU.mult, None, [rPB5q[2], r_scl], [r_tmpX])
                    stt(S[:], S[:], scl[:, 9:10], tmpX[:], ALU.mult, ALU.add, [rS, r_scl, r_tmpX, r_Sp[0]], [rS])
                    actf(Sp[1][:], S[:], AF.Identity, [rS, r_scl], [r_Sp[1]], scale=scl[:, 10:11])
                    ts(dve, tmpX[:], PB[4][:, 384:512], scl[:, 13:14], None, ALU.mult, None, [rPB[4], r_scl], [r_tmpX])
                    stt(S[:], S[:], scl[:, 11:12], tmpX[:], ALU.mult, ALU.add, [rS, r_scl, r_tmpX, r_Sp[1]], [rS])
                    mm(PB[5][:, 128:256], AT_sb[:], vh, True, False, [r_AT, r_v], [rPB5q[1]])
                    mm(PB[5][0:64, 128:256], qsT[:, 0:64], Sp[0][:], False, False, [r_qsT, r_Sp[0]], [rPB5q[1]])
                    mm(PB[5][64:128, 128:256], qsT[:, 64:128], Sp[1][:], False, True, [r_qsT, r_Sp[1]], [rPB5q[1]])
                    actf(ojunk[:], PB[5][:, 128:256], AF.Square, [rPB5q[1]], [r_ojunk, r_oss], accum=oss[:, 0:1])
                    ts(pool, oss[:, 1:2], oss[:, 0:1], 1.0 / 128, EPS, ALU.mult, ALU.add, [r_oss], [r_oss])
                    tt(pool, oss[:, 2:3], oss[:, 1:2], negh[:, 0:1], ALU.pow, [r_oss, r_cf], [r_oss])
                    stt(yhg[:, hh * 128:(hh + 1) * 128], PB[5][:, 128:256], oss[:, 2:3], gateA[:, hh * 128:(hh + 1) * 128],
                        ALU.mult, ALU.mult, [rPB5q[1], r_oss, r_gateA], [r_yhg])
                for a in range(2):
                    tr(PT[1][:, 768 + a * 128:768 + (a + 1) * 128], yhg[:, a * 128:(a + 1) * 128], [r_yhg], [rPT[1]])
                cp(dve, yTt[:], PT[1][:, 768:1024].rearrange("p (a t) -> p a t", t=128), [rPT[1]], [r_yTt])
                fw.dma(sp, yT_out[0:2, :, qt * 128:(qt + 1) * 128].rearrange("c p t -> p c t"), yTt[:], reads=[r_yTt],
                       writes=[r_yTo[qt]], sem_res=r_yTt)

            rPB5q = [rPB[5]] * 4
            r_NZd = [Res("NZd%d" % i) for i in range(NT)]
            r_QTd = [Res("QTd%d" % i) for i in range(NT)]
            r_yTo = [Res("yTo%d" % i) for i in range(NT)]
            r_yTo2 = [Res("yTo2_%d" % i) for i in range(NT)]

            hid = fw.sb("hid", [128, 256], BF16); r_hid = Res("hid")
            cbias = fw.sb("cbias", [128, 2], F32); r_cbias = Res("cbias")
            kcs = fw.sb("kcs", [128, 2, 64], F32); r_kcs = Res("kcs")
            kcb = fw.sb("kcb", [128, 2, 64], BF16); r_kcb = Res("kcb")
            css = fw.sb("css", [128, 8], F32); r_css = Res("css")
            cjunk = fw.sb("cjunk", [128, 64], F32); r_cjunk = Res("cjunk")

            def compress():
                allKT = r_KVC
                cp(dve, VCX[:, :, 65:129], cf[:, CF_OV:CF_OV + 128].rearrange("p (a j) -> p a j", j=64), [r_cf], [r_VCX])
                fw.op(pool, lambda e: e.memset(VCX[:, :, 64:65], 1.0), [], [r_VCX])
                fw.op(pool, lambda e: e.memset(KCT[:], 0.0), [], [r_KCT])
                fw.op(pool, lambda e: e.memset(css[:], 1.0), [], [r_css])
                for x in range(2):
                    for l in range(32):
                        mm(PB[1][:, 0:1], W1[:, x, l, :], peT[:, x, l:l + 1], l == 0, l == 31, [r_W1, r_peT], [rPB[1]])
                    cp(dve, cbias[:, x:x + 1], PB[1][:, 0:1], [rPB[1]], [r_cbias])
                    for l in range(32):
                        mm(PB[0][:, 0:255], W1[:, x, l, :], KVC[:, x, l:l + 16 * 254 + 1:16], l == 0, l == 31,
                           [r_W1] + allKT, [rPB[0]])
                    actf(hid[:, 0:255], PB[0][:, 0:255], AF.Silu, [rPB[0], r_cbias], [r_hid], bias=cbias[:, x:x + 1])
                    for ct in range(2):
                        n_ = 128 if ct == 0 else 127
                        mm(PB[2][0:n_, ct * 64:(ct + 1) * 64], hid[:, ct * 128:ct * 128 + n_], W2[:, x, :], True, True,
                           [r_hid, r_W2], [rPB[2]])
                    if x == 0:
                        for ct in range(2):
                            n_ = 128 if ct == 0 else 127
                            actf(cjunk[0:n_, :], PB[2][0:n_, ct * 64:(ct + 1) * 64], AF.Square, [rPB[2]], [r_cjunk, r_css],
                                 accum=css[0:n_, ct:ct + 1])
                        ts(pool, css[:, 2:4], css[:, 0:2], 1.0 / 64, EPS, ALU.mult, ALU.add, [r_css], [r_css])
                        tt(pool, css[:, 4:6], css[:, 2:4], negh[:, 0:2], ALU.pow, [r_css, r_cf], [r_css])
                        for ct in range(2):
                            n_ = 128 if ct == 0 else 127
                            stt(kcb[0:n_, ct, :], PB[2][0:n_, ct * 64:(ct + 1) * 64], css[0:n_, 4 + ct:5 + ct], gqk[0:n_, 6, :],
                                ALU.mult, ALU.mult, [rPB[2], r_css, r_gqk], [r_kcb])
                            tr(PT[0][0:64, ct * 128:ct * 128 + n_], kcb[0:n_, ct, :], [r_kcb], [rPT[0]])
                            cp(dve, KCT[:, ct * 128:ct * 128 + n_], PT[0][0:64, ct * 128:ct * 128 + n_], [rPT[0]], [r_KCT])
                    else:
                        for ct in range(2):
                            n_ = 128 if ct == 0 else 127
                            cp(dve, VCX[0:n_, ct, 0:64], PB[2][0:n_, ct * 64:(ct + 1) * 64], [rPB[2]], [r_VCX])

            qts = [fw.sb("qts%d" % i, [128, 512], BF16) for i in range(2)]
            r_qts = [Res("qts%d" % i) for i in range(2)]
            nzs = [fw.sb("nzs%d" % i, [128, 256], BF16) for i in range(2)]
            r_nzs = [Res("nzs%d" % i) for i in range(2)]
            Ecf = fw.sb("Ecf", [128, 512], BF16); r_Ecf = Res("Ecf")
            Ecm = fw.sb("Ecm", [128, 512], BF16); r_Ecm = Res("Ecm")
            Es = [fw.sb("Es%d" % i, [128, 512], BF16) for i in range(3)]
            r_Es = [Res("Es%d" % i) for i in range(3)]
            Ew = [fw.sb("Ew%d" % i, [128, 512], BF16) for i in range(2)]
            r_Ew = [Res("Ew%d" % i) for i in range(2)]
            cmps = fw.sb("cmps", [128, 4, 129], F32); r_cmps = Res("cmps")
            brs = fw.sb("brs", [128, 4, 65], F32); r_brs = Res("brs")
            rc = fw.sb("rc", [128, 16], F32); r_rc = Res("rc")
            imp = fw.sb("imp", [128, 64], F32); r_imp = Res("imp")
            score = fw.sb("score", [128, 64], F32); r_score = Res("score")
            score2 = fw.sb("score2", [128, 64], F32); r_score2 = Res("score2")
            m8 = fw.sb("m8", [128, 16], F32); r_m8 = Res("m8")
            negsel = fw.sb("negsel", [128, 64], BF16); r_negsel = Res("negsel")
            onsa = fw.sb("onsa", [128, 4, 64], F32); r_onsa = Res("onsa")
            otmp = fw.sb("otmp", [128, 4, 64], F32); r_otmp = Res("otmp")
            ynsa = fw.sb("ynsa", [128, 256], BF16); r_ynsa = Res("ynsa")
            yTn = fw.sb("yTn", [128, 2, 128], BF16); r_yTn = Res("yTn")

            def A2_load(qt):
                i = qt % 2
                fw.dma(sp, qts[i][:], QT_d[qt, :, :], reads=[r_QTd[qt]], writes=[r_qts[i]], sem_res=r_qts[i])
                fw.dma(sp, nzs[i][:], NZ_d[qt, :, :], reads=[r_NZd[qt]], writes=[r_nzs[i]], sem_res=r_nzs[i])

            def A2_tile(qt):
                i = qt % 2
                Q = qts[i]
                rQ = r_qts[i]
                ncts = 1 if (8 * qt + 6) < 128 else 2
                for ct in range(ncts):
                    n_ = 128 if ct == 0 else 127
                    mm(PB[0][0:n_, :], KCT[0:64, ct * 128:ct * 128 + n_], Q[0:64, :], True, True, [r_KCT, rQ], [rPB[0]])
                    fully = (ct == 0 and 8 * qt - 2 >= 127)
                    if fully:
                        actf(Ecm[0:n_, :], PB[0][0:n_, :], AF.Exp, [rPB[0]], [r_Ecm])
                    else:
                        actf(Ecf[0:n_, :], PB[0][0:n_, :], AF.Exp, [rPB[0]], [r_Ecf])
                        thr = float(128 * qt - 2048 * ct)
                        stt(Ecm[0:n_, :], cf[0:n_, CF_VCMP:CF_VCMP + 512], thr, Ecf[0:n_, :], ALU.is_le, ALU.mult,
                            [r_cf, r_Ecf], [r_Ecm])
                    for n in range(4):
                        bank = 1 + n // 2
                        mm(PB[bank][:, (n % 2) * 129:(n % 2) * 129 + 129], Ecm[0:n_, n * 128:(n + 1) * 128], VCX[0:n_, ct, :],
                           (ct == 0 and n % 2 == 0), (ct == ncts - 1), [r_Ecm, r_VCX], [rPB[bank]])
                cp(act, cmps[:, 0:2, :], PB[1][:, 0:258].rearrange("p (a d) -> p a d", d=129), [rPB[1]], [r_cmps])
                cp(act, cmps[:, 2:4, :], PB[2][:, 0:258].rearrange("p (a d) -> p a d", d=129), [rPB[2]], [r_cmps])
                ts(dve, rc[:, 0:4], cmps[:, :, 64], 1e-30, None, ALU.max, None, [r_cmps], [r_rc])
                fw.op(dve, lambda e: e.reciprocal(rc[:, 4:8], rc[:, 0:4]), [r_rc], [r_rc])
                ts(dve, imp[:], cmps[:, 0, 65:129], rc[:, 4:5], None, ALU.mult, None, [r_cmps, r_rc], [r_imp])
                for n in range(1, 4):
                    stt(imp[:], cmps[:, n, 65:129], rc[:, 4 + n:5 + n], imp[:], ALU.mult, ALU.add, [r_cmps, r_rc, r_imp], [r_imp])
                o0 = 62 - 2 * qt
                tt(dve, score[:], imp[:], cf[:, CF_ETMPL + o0:CF_ETMPL + o0 + 64], ALU.mult, [r_imp, r_cf], [r_score])
                tt(dve, score[:], score[:], cf[:, CF_CTMPL + o0:CF_CTMPL + o0 + 64], ALU.add, [r_score, r_cf], [r_score])
                ts(dve, score[:, 0:1], score[:, 0:1], 1e4, None, ALU.add, None, [r_score], [r_score])
                fw.op(dve, lambda e: e.max(m8[:, 0:8], score[:]), [r_score], [r_m8])
                fw.op(dve, lambda e: e.match_replace(score2[:], m8[:, 0:8], score[:], -1e30), [r_score, r_m8], [r_score2])
                fw.op(dve, lambda e: e.max(m8[:, 8:16], score2[:]), [r_score2], [r_m8])
                ts(dve, negsel[:], score[:], m8[:, 15:16], NEG, ALU.is_lt, ALU.mult, [r_score, r_m8], [r_negsel])
                tr(PT[0][0:64, 0:128], negsel[:], [r_negsel], [rPT[0]])
                cp(dve, Q[0:64, :].rearrange("p (a t) -> p a t", t=128), PT[0][0:64, 0:128].unsqueeze(1).broadcast_to([64, 4, 128]),
                   [rPT[0]], [rQ])
                tt(dve, rc[:, 8:12], rc[:, 4:8], GATES[:, qt, 0:4], ALU.mult, [r_rc, r_G[qt]], [r_rc])
                tt(dve, onsa[:], cmps[:, :, 0:64], rc[:, 8:12].unsqueeze(2).broadcast_to([128, 4, 64]), ALU.mult,
                   [r_cmps, r_rc], [r_onsa])
                for kt in range(qt + 1):
                    bank = 0 if kt % 2 == 0 else 3
                    e_i = kt % 3
                    diag = (kt == qt)
                    mm(PB[bank][:, :], KT[:, 0, kt * 128:(kt + 1) * 128], Q[:, :], True, not diag, [r_KT[kt], r_KTc, rQ], [rPB[bank]])
                    if diag:
                        mm(PB[bank][:, :], ident, cb[:, CB_CAUS:CB_CAUS + 512], False, True, [r_cb], [rPB[bank]])
                    actf(Es[e_i][:], PB[bank][:, :], AF.Exp, [rPB[bank]], [r_Es[e_i]])
                    for n in range(4):
                        mm(PB[4][:, n * 65:(n + 1) * 65], Es[e_i][:, n * 128:(n + 1) * 128], V_all[:, kt, 0, :],
                           (kt == 0 and n == 0), (kt == qt), [r_Es[e_i], r_V[kt], r_Vones], [rPB[4]])
                k0 = max(0, qt - 4)
                for kt in range(k0, qt + 1):
                    e_i = kt % 2
                    first = (kt == qt - 4)
                    diag = (kt == qt)
                    mm(PB[5][:, :], KT[:, 1, kt * 128:(kt + 1) * 128], Q[:, :], True, not (first or diag),
                       [r_KT[kt], r_KTc, rQ], [rPB[5]])
                    if first:
                        mm(PB[5][:, :], ident, cb[:, CB_WIN:CB_WIN + 512], False, True, [r_cb], [rPB[5]])
                    if diag:
                        mm(PB[5][:, :], ident, cb[:, CB_CAUS:CB_CAUS + 512], False, True, [r_cb], [rPB[5]])
                    actf(Ew[e_i][:], PB[5][:, :], AF.Exp, [rPB[5]], [r_Ew[e_i]])
                    for n in range(4):
                        mm(PB[1][:, n * 65:(n + 1) * 65], Ew[e_i][:, n * 128:(n + 1) * 128], V_all[:, kt, 1, :],
                           (kt == k0 and n == 0), (kt == qt), [r_Ew[e_i], r_V[kt], r_Vones], [rPB[1]])
                for br, bank in ((1, 4), (2, 1)):
                    cp(act, brs[:], PB[bank][:, 0:260].rearrange("p (a d) -> p a d", d=65), [rPB[bank]], [r_brs])
                    fw.op(dve, lambda e: e.reciprocal(rc[:, 12:16], brs[:, :, 64]), [r_brs], [r_rc])
                    tt(dve, rc[:, 12:16], rc[:, 12:16], GATES[:, qt, br * 4:br * 4 + 4], ALU.mult, [r_rc, r_G[qt]], [r_rc])
                    tt(dve, otmp[:], brs[:, :, 0:64], rc[:, 12:16].unsqueeze(2).broadcast_to([128, 4, 64]), ALU.mult,
                       [r_brs, r_rc], [r_otmp])
                    tt(pool, onsa[:], onsa[:], otmp[:], ALU.add, [r_onsa, r_otmp], [r_onsa])
                tt(dve, ynsa[:], onsa[:].rearrange("p a d -> p (a d)"), nzs[i][:], ALU.mult, [r_onsa, r_nzs[i]], [r_ynsa])
                for a in range(2):
                    tr(PT[1][:, a * 128:(a + 1) * 128], ynsa[:, a * 128:(a + 1) * 128], [r_ynsa], [rPT[1]])
                cp(dve, yTn[:], PT[1][:, 0:256].rearrange("p (a t) -> p a t", t=128), [rPT[1]], [r_yTn])
                fw.dma(sp, yT_out[2:4, :, qt * 128:(qt + 1) * 128].rearrange("c p t -> p c t"), yTn[:], reads=[r_yTn],
                       writes=[r_yTo2[qt]], sem_res=r_yTn)

        def h_load(qt):
            fw.dma(sp, h_sb[qt % 2][:], h_in[qt * 128:(qt + 1) * 128, :], writes=[r_h[qt % 2]])

        if doB:
            load_WB()
        if doA:
            load_WA()
            load_cmp_w()
        h_load(0)
        if doB:
            B_load(0)
        for qt in range(NT):
            if qt + 1 < NT:
                h_load(qt + 1)
                if doB:
                    B_load(qt + 1)
            if doB:
                B_tile(qt)
            if doA:
                A1_tile(qt, h_sb[qt % 2], r_h[qt % 2])
        if doA:
            compress()
            A2_load(0)
            for qt in range(NT):
                if qt + 1 < NT:
                    A2_load(qt + 1)
                A2_tile(qt)
        final = []
        outs = []
        if doB:
            outs += r_h
        if doA:
            outs += [r_yTt, r_yTn]
        for r in outs:
            if r.dsem is not None:
                final.append((r.dsem, r.dsem.count))
        fw.wait_tokens(sp, final)
        fw.emit()
    return nc, fw.n_ops


_PROGS = {}


def _get_prog(doB, doA):
    key = (doB, doA)
    if key not in _PROGS:
        _PROGS[key] = build_pass(doB, doA)[0]
    return _PROGS[key]


def _prepA(inputs, layer, g):
    f32 = np.float32
    d = {}
    d["win"] = np.ascontiguousarray(inputs["w_in"][layer][:, _in_cols(g)], dtype=f32)
    avec = np.zeros((128, 32), f32)
    avec[:, 0:8] = inputs["norm_g"][layer].reshape(8, 128).T
    lb = inputs["hgrn_lb"]
    for hh in range(2):
        H = 2 * g + hh
        avec[:, 8 + 4 * hh:12 + 4 * hh] = lb[:, H * 128:(H + 1) * 128].T
    lsel = np.zeros(4, f32)
    lsel[1:layer + 1] = 1.0
    avec[:, 16:20] = lsel[None, :]
    d["avec"] = avec
    gq = np.concatenate([np.tile(inputs["nsa_qnorm_g"][layer], 4), inputs["nsa_knorm_g"][layer, 1],
                         inputs["nsa_knorm_g"][layer, 2], inputs["nsa_knorm_g"][layer, 0]]).astype(f32)
    d["gqk"] = np.ascontiguousarray(np.tile(gq[None, :], (128, 1)))
    d["w1"] = np.ascontiguousarray(inputs["cmp_w1"][layer], dtype=f32)
    d["w2"] = np.ascontiguousarray(inputs["cmp_w2"][layer].transpose(1, 0, 2), dtype=f32)
    d["peT"] = np.ascontiguousarray(inputs["cmp_pe"][layer].transpose(2, 0, 1), dtype=f32)
    return d


def _prepB(inputs, layer, b):
    f32 = np.float32
    d = {}
    d["wout"] = np.ascontiguousarray(inputs["w_out"][layer][_WOUT_ROWS, :], dtype=f32)
    d["wpg"] = np.ascontiguousarray(inputs["w_pg"][layer], dtype=f32)
    d["wpp"] = np.ascontiguousarray(inputs["w_pp"][layer], dtype=f32)
    bvec = np.zeros((128, 16), f32)
    bvec[:, 0] = inputs["hgrn_onorm_g"][layer]
    bvec[:, 8:16] = inputs["ple_norm_g"][layer].reshape(8, 128).T
    d["bvec"] = bvec
    d["p_in"] = np.ascontiguousarray(inputs["p"][layer, b], dtype=f32)
    return d


def kernel(**inputs):
    inputs = {k: np.asarray(v) for k, v in inputs.items()}
    cf, cb = _const_tables()
    depth = inputs["w_in"].shape[0]
    B = inputs["x"].shape[0]
    h = [np.ascontiguousarray(inputs["x"][b], dtype=np.float32) for b in range(B)]
    yT = None
    for ps in range(depth + 1):
        doB = ps > 0
        doA = ps < depth
        nc = _get_prog(doB, doA)
        in_maps = []
        for c in range(8):
            b, g = c // 2, c % 2
            m = {"h_in": h[b], "cf": cf, "cb": cb}
            if doB:
                m.update(_prepB(inputs, ps - 1, b))
                m["yT_in"] = yT[b]
            if doA:
                m.update(_prepA(inputs, ps, g))
            in_maps.append(m)
        res = run_bass_kernel_spmd(nc, in_maps, core_ids=list(range(8)))
        if doB:
            h = [np.asarray(res.results[2 * b]["h_out"], dtype=np.float32) for b in range(B)]
        if doA:
            yT = [np.ascontiguousarray(np.concatenate([np.asarray(res.results[2 * b]["yT_out"]),
                                                       np.asarray(res.results[2 * b + 1]["yT_out"])], axis=0))
                  for b in range(B)]
    return np.stack(h, axis=0).astype(np.float32)
```
